# Optimizing a Trainium2 kernel written in Bass

```python
import jax, jax.numpy as jnp
from jax import lax
import numpy as np

D_MODEL = 4096
BATCH = 2
SEQ = 8192
DEPTH = 4

ATT_HEADS = 8
ATT_HEAD_DIM = 128
ATT_W = ATT_HEADS * ATT_HEAD_DIM
Q_BLOCK = 128
CONV_W = 1024
CONV_K = 3
RWKV_HEADS = 16
RWKV_HEAD_DIM = 64
RWKV_W = RWKV_HEADS * RWKV_HEAD_DIM
DECAY_LORA = 64
AAA_LORA = 64
GATE_LORA = 160
RWKV_SHIFT_W = 3 * RWKV_W + DECAY_LORA + AAA_LORA + GATE_LORA
N_BRANCH = 3
MERGE_RANK = 256
D_FF = 4 * D_MODEL
RMS_EPS = 1e-6
LNX_EPS = 64e-5
KK_EPS = 1e-12

IN_SPLIT_SIZES = (ATT_W, ATT_W, ATT_W, CONV_W, CONV_W, CONV_W, RWKV_SHIFT_W, MERGE_RANK)
IN_COLS = sum(IN_SPLIT_SIZES)
RWKV_SPLIT_SIZES = (RWKV_W, RWKV_W, RWKV_W, DECAY_LORA, AAA_LORA, GATE_LORA)

kernel_name = 'hybrid_sb_conv_rwkv7_trunk'


def rmsnorm(x, g):
    xf = x.astype(jnp.float32)
    y = xf * lax.rsqrt(jnp.mean(xf * xf, axis=-1, keepdims=True) + RMS_EPS)
    return (y * g.astype(jnp.float32)).astype(x.dtype)


def split_cols(x, sizes):
    offsets = np.cumsum(np.array(sizes))[:-1].tolist()
    return jnp.split(x, offsets, axis=-1)


def stick_breaking_attention(q, k, v):
    b, s, h, d = q.shape
    nb = s // Q_BLOCK
    scale = d ** -0.5
    qb = q.reshape(b, nb, Q_BLOCK, h, d).transpose(1, 0, 3, 2, 4)
    kh = k.transpose(0, 2, 1, 3)
    vh = v.transpose(0, 2, 1, 3)
    key_pos = jnp.arange(s, dtype=jnp.int32)

    def block(args):
        q_blk, start = args
        z = jnp.einsum('bhqd,bhkd->bhqk', q_blk, kh).astype(jnp.float32) * scale
        q_pos = start + jnp.arange(Q_BLOCK, dtype=jnp.int32)
        causal = key_pos[None, :] < q_pos[:, None]
        sp = jnp.where(causal, jax.nn.softplus(z), 0.0)
        after = lax.cumsum(sp, axis=3, reverse=True) - sp
        w = jnp.where(causal, jnp.exp(jax.nn.log_sigmoid(z) - after), 0.0)
        return jnp.einsum('bhqk,bhkd->bhqd', w.astype(v.dtype), vh)

    starts = jnp.arange(nb, dtype=jnp.int32) * Q_BLOCK
    o = lax.map(block, (qb, starts))
    return o.transpose(1, 0, 3, 2, 4).reshape(b, s, h * d)


def short_gated_conv(b_gate, c_gate, u, conv_w):
    y = lax.conv_general_dilated(c_gate * u, conv_w[:, None, :], window_strides=(1,),
                                 padding=((CONV_K - 1, 0),),
                                 dimension_numbers=('NWC', 'WIO', 'NWC'),
                                 feature_group_count=CONV_W)
    return b_gate * y


def token_shift(p, mu):
    prev = jnp.pad(p, ((0, 0), (1, 0), (0, 0)))[:, :-1]
    return p + mu * (prev - p)


def wkv7_scan(r, decay, k, v, a_vec, b_vec):
    f32 = jnp.float32
    xs = tuple(jnp.moveaxis(t.astype(f32), 1, 0) for t in (r, decay, k, v, a_vec, b_vec))
    bsz, _, h, n = r.shape

    def step(state, inp):
        r_t, w_t, k_t, v_t, a_t, b_t = inp
        sa = jnp.einsum('bhvk,bhk->bhv', state, a_t)
        state = (state * w_t[:, :, None, :] + sa[..., None] * b_t[:, :, None, :]
                 + v_t[..., None] * k_t[:, :, None, :])
        return state, jnp.einsum('bhvk,bhk->bhv', state, r_t)

    s0 = jnp.zeros((bsz, h, n, n), f32)
    _, y = lax.scan(step, s0, xs)
    return jnp.moveaxis(y, 0, 1)


def rwkv7_time_mix(seg, mu, w0, w_decay_up, a0, w_a_up, w_g_up, k_k, k_a, r_k, lnx_w, lnx_b):
    b, s, _ = seg.shape
    seg = token_shift(seg, mu)
    r, k, v, lw, la, lg = split_cols(seg, RWKV_SPLIT_SIZES)
    heads = lambda t: t.reshape(b, s, RWKV_HEADS, RWKV_HEAD_DIM)
    w = -jax.nn.softplus(-(w0 + jnp.tanh(lw) @ w_decay_up)) - 0.5
    decay = jnp.exp(-jnp.exp(w.astype(jnp.float32)))
    a = jax.nn.sigmoid(a0 + la @ w_a_up)
    g = jax.nn.sigmoid(lg) @ w_g_up
    kk = heads(k * k_k).astype(jnp.float32)
    kk = kk / jnp.maximum(jnp.sqrt(jnp.sum(kk * kk, axis=-1, keepdims=True)), KK_EPS)
    k = k * (1.0 + (a - 1.0) * k_a)
    y = wkv7_scan(heads(r), heads(decay), heads(k), heads(v), -kk, kk * heads(a))
    mean = jnp.mean(y, axis=-1, keepdims=True)
    var = jnp.mean(jnp.square(y - mean), axis=-1, keepdims=True)
    yn = ((y - mean) * lax.rsqrt(var + LNX_EPS)).reshape(b, s, RWKV_W) * lnx_w + lnx_b
    bonus = jnp.sum(heads(r) * heads(k) * r_k, axis=-1, keepdims=True) * heads(v)
    return ((yn + bonus.reshape(b, s, RWKV_W)) * g).astype(seg.dtype)


def setup_inputs(seed: int = 0) -> dict:
    key = jax.random.key(seed)
    ks = list(jax.random.split(key, 25))
    L = DEPTH
    f32 = jnp.float32
    nrm = lambda i, shape, scale: jax.random.normal(ks[i], shape, f32) * scale
    uni = lambda i, shape, lo, hi: jax.random.uniform(ks[i], shape, f32, lo, hi)
    return {
        'x': nrm(0, (BATCH, SEQ, D_MODEL), 1.0),
        'norm_mix': 1.0 + nrm(1, (L, D_MODEL), 0.02),
        'w_in': nrm(2, (L, D_MODEL, IN_COLS), D_MODEL ** -0.5),
        'w_att_o': nrm(3, (L, ATT_W, D_MODEL), ATT_W ** -0.5),
        'conv_w': nrm(4, (L, CONV_K, CONV_W), CONV_K ** -0.5),
        'w_conv_o': nrm(5, (L, CONV_W, D_MODEL), CONV_W ** -0.5),
        'rwkv_mu': uni(6, (L, RWKV_SHIFT_W), 0.0, 1.0),
        'rwkv_w0': uni(7, (L, RWKV_W), -6.0, -1.0),
        'rwkv_w_decay_up': nrm(8, (L, DECAY_LORA, RWKV_W), 0.5 * DECAY_LORA ** -0.5),
        'rwkv_a0': nrm(9, (L, RWKV_W), 0.5),
        'rwkv_w_a_up': nrm(10, (L, AAA_LORA, RWKV_W), 0.5 * AAA_LORA ** -0.5),
        'rwkv_w_g_up': nrm(11, (L, GATE_LORA, RWKV_W), GATE_LORA ** -0.5),
        'rwkv_k_k': 0.85 + nrm(12, (L, RWKV_W), 0.02),
        'rwkv_k_a': 1.0 + nrm(13, (L, RWKV_W), 0.02),
        'rwkv_r_k': nrm(14, (L, RWKV_HEADS, RWKV_HEAD_DIM), 0.1),
        'rwkv_lnx_w': 1.0 + nrm(15, (L, RWKV_W), 0.02),
        'rwkv_lnx_b': nrm(16, (L, RWKV_W), 0.02),
        'w_rwkv_o': nrm(17, (L, RWKV_W, D_MODEL), RWKV_W ** -0.5),
        'w_gate_up': nrm(18, (L, MERGE_RANK, N_BRANCH * D_MODEL), MERGE_RANK ** -0.5),
        'b_gate': nrm(19, (L, N_BRANCH * D_MODEL), 0.02),
        'w_out': nrm(20, (L, D_MODEL, D_MODEL), D_MODEL ** -0.5),
        'norm_mlp': 1.0 + nrm(21, (L, D_MODEL), 0.02),
        'w_mlp_up': nrm(22, (L, D_MODEL, D_FF), D_MODEL ** -0.5),
        'w_mlp_down': nrm(23, (L, D_FF, D_MODEL), D_FF ** -0.5),
        'norm_final': 1.0 + nrm(24, (D_MODEL,), 0.02),
    }


def reference(x, norm_mix, w_in, w_att_o, conv_w, w_conv_o, rwkv_mu, rwkv_w0, rwkv_w_decay_up,
              rwkv_a0, rwkv_w_a_up, rwkv_w_g_up, rwkv_k_k, rwkv_k_a, rwkv_r_k, rwkv_lnx_w,
              rwkv_lnx_b, w_rwkv_o, w_gate_up, b_gate, w_out, norm_mlp, w_mlp_up, w_mlp_down,
              norm_final):
    b, s, _ = x.shape
    att_heads = lambda t: t.reshape(b, s, ATT_HEADS, ATT_HEAD_DIM)
    for l in range(DEPTH):
        h = rmsnorm(x, norm_mix[l])
        p = h @ w_in[l]
        q, k, v, c_b, c_c, c_u, rw, gate_down = split_cols(p, IN_SPLIT_SIZES)
        y_att = stick_breaking_attention(att_heads(q), att_heads(k), att_heads(v)) @ w_att_o[l]
        y_conv = short_gated_conv(c_b, c_c, c_u, conv_w[l]) @ w_conv_o[l]
        y_rwkv = rwkv7_time_mix(rw, rwkv_mu[l], rwkv_w0[l], rwkv_w_decay_up[l], rwkv_a0[l],
                                rwkv_w_a_up[l], rwkv_w_g_up[l], rwkv_k_k[l], rwkv_k_a[l],
                                rwkv_r_k[l], rwkv_lnx_w[l], rwkv_lnx_b[l]) @ w_rwkv_o[l]
        gates = jax.nn.sigmoid(gate_down @ w_gate_up[l] + b_gate[l])
        g_att, g_conv, g_rwkv = jnp.split(gates, N_BRANCH, axis=-1)
        x = x + (g_att * y_att + g_conv * y_conv + g_rwkv * y_rwkv) @ w_out[l]
        h = rmsnorm(x, norm_mlp[l])
        x = x + jnp.square(jax.nn.relu(h @ w_mlp_up[l])) @ w_mlp_down[l]
    return rmsnorm(x, norm_final)
```

```python
from contextlib import ExitStack
import numpy as np
import concourse.bass as bass
import concourse.mybir as mybir
from concourse.bass_utils import run_bass_kernel_spmd

F32 = mybir.dt.float32
BF16 = mybir.dt.bfloat16
AF = mybir.ActivationFunctionType
ALU = mybir.AluOpType

AW = 1024
LORA_D, LORA_A, LORA_G = 64, 64, 160
MR = 256
IN_COLS = 9 * AW + LORA_D + LORA_A + LORA_G + MR
OFF_LW = 9 * AW
OFF_LG = OFF_LW + 128
OFF_GD = OFF_LG + LORA_G
RMS_EPS, LNX_EPS = 1e-6, 64e-5
NREP = 16


class Cfg:
    def __init__(s, D=4096, T=8192, DFF=16384, L=4, B=2, G=4, slab=(256, 2048)):
        s.D, s.T, s.DFF, s.L, s.B, s.G = D, T, DFF, L, B, G
        s.NC = B * G
        s.multi = s.NC > 1
        s.KC = D // 128
        s.TQ = T // G
        s.TT = 512
        s.Wc = AW // G
        s.GWc = MR // G
        s.nch = s.Wc // 128
        s.NRB = 3 * s.nch + (s.GWc + 127) // 128
        s.CW = 9 * s.Wc + 128 + 256 + s.GWc
        s.cLWLA, s.cLG, s.cGD = 9 * s.Wc, 9 * s.Wc + 128, 9 * s.Wc + 384
        s.slab = slab
        s.SLAB = slab[0] * slab[1]
        s.wdefs = [("w_in", D, s.CW, 2), ("w_att_o", AW, D, 8), ("w_conv_o", AW, D, 8), ("w_rwkv_o", AW, D, 8),
                   ("w_gate_up", MR, 3 * D, 8), ("w_out", D, D, 8), ("w_mlp_up", D, DFF, 8), ("w_mlp_down", DFF, D, 8)]

    def wpad(s, r, c, ways):
        unit = s.SLAB * (ways if s.multi else 1)
        return ((r * c + unit - 1) // unit) * unit


def sp_layout(cfg):
    KC, nch = cfg.KC, cfg.nch
    cols, n = {}, 0
    for name, k in (("gmix", KC), ("gmlp", KC), ("bgate", 3 * KC), ("convw", 3 * nch), ("mu_r", nch), ("mu_k", nch),
                    ("mu_v", nch), ("mu_l", 1), ("mu_g", 2), ("w0", nch), ("a0", nch), ("k_k", nch), ("k_a", nch),
                    ("lnx_w", nch), ("lnx_b", nch), ("r_k", nch), ("negw0", nch), ("omka", nch)):
        cols[name] = n
        n += k
    return cols, n


def col128(v):
    v = np.asarray(v, np.float32)
    return np.ascontiguousarray(v.reshape(-1, 128).T)


def pack_small(cfg, inp, l, g):
    cols, n = sp_layout(cfg)
    A = np.zeros((128, n), np.float32)
    W = cfg.Wc
    def put(name, arr):
        A[:, cols[name]:cols[name] + arr.shape[1]] = arr
    put("gmix", col128(inp["norm_mix"][l])); put("gmlp", col128(inp["norm_mlp"][l]))
    put("bgate", col128(inp["b_gate"][l]))
    cw = np.asarray(inp["conv_w"][l])[:, g * W:(g + 1) * W]
    put("convw", np.concatenate([col128(cw[j]) for j in range(3)], axis=1))
    mu = np.asarray(inp["rwkv_mu"][l])
    for i, nm in enumerate(("mu_r", "mu_k", "mu_v")):
        put(nm, col128(mu[i * AW + g * W:i * AW + (g + 1) * W]))
    put("mu_l", mu[3 * AW:3 * AW + 128].reshape(128, 1))
    mg = np.zeros((256,), np.float32); mg[:LORA_G] = mu[3 * AW + 128:3 * AW + 128 + LORA_G]
    put("mu_g", col128(mg))
    for nm, key in (("w0", "rwkv_w0"), ("a0", "rwkv_a0"), ("k_k", "rwkv_k_k"), ("k_a", "rwkv_k_a"),
                    ("lnx_w", "rwkv_lnx_w"), ("lnx_b", "rwkv_lnx_b"), ("r_k", "rwkv_r_k")):
        put(nm, col128(np.asarray(inp[key][l]).reshape(-1)[g * W:(g + 1) * W]))
    return A


def pack_lora(cfg, inp, l, g):
    W = cfg.Wc
    sl = slice(g * W, (g + 1) * W)
    lu = np.zeros((128, W), np.float32)
    lu[0:64] = np.asarray(inp["rwkv_w_decay_up"][l])[:, sl]
    lu[64:128] = np.asarray(inp["rwkv_w_a_up"][l])[:, sl]
    gu = np.asarray(inp["rwkv_w_g_up"][l])
    gb = np.zeros((128, W), np.float32)
    gb[0:32] = gu[128:160, sl]
    return np.concatenate([lu, gu[0:128, sl], gb], axis=1)


def make_consts():
    c = {}
    k = np.arange(128)[:, None]
    j = np.arange(128)[None, :]
    c["ident"] = np.eye(128, dtype=np.float32)
    c["ustrict"] = (k > j).astype(np.float32)
    c["lincl"] = (k <= j).astype(np.float32)
    c["blk64"] = ((k // 64) == (j // 64)).astype(np.float32)
    q = np.arange(512)[None, :]
    c["cmask"] = np.concatenate([(q > (k + 128 * mi)).astype(np.float32) for mi in range(4)], axis=1)
    return c


def win_core(cfg, w, g):
    W, GW = cfg.Wc, cfg.GWc
    parts = [w[:, i * AW + g * W:i * AW + (g + 1) * W] for i in range(9)]
    parts.append(w[:, OFF_LW:OFF_LW + 128])
    parts.append(w[:, OFF_LG:OFF_LG + LORA_G])
    parts.append(np.zeros((w.shape[0], 256 - LORA_G), np.float32))
    parts.append(w[:, OFF_GD + g * GW:OFF_GD + (g + 1) * GW])
    return np.concatenate(parts, axis=1)


class Buf:
    __slots__ = ("w", "r")

    def __init__(s):
        s.w = None
        s.r = {}


class Emit:
    NDS = 16

    def __init__(s, nc):
        s.nc = nc
        s.eng = {"pe": nc.tensor, "act": nc.scalar, "dve": nc.vector, "pool": nc.gpsimd, "sp": nc.sync}
        s.csem, s.ccount, s.dsem, s.dk = {}, {}, {}, {}
        s.waited = {e: {} for e in s.eng}
        s.pending = {e: [] for e in s.eng}
        for e in ("pe", "act", "dve", "pool"):
            s.csem[e] = nc.alloc_semaphore("c_" + e)
            s.ccount[e] = 0
        for q in ("sp", "pool"):
            s.dsem[q] = [nc.alloc_semaphore("d_%s%d" % (q, i)) for i in range(s.NDS)]
            s.dk[q] = 0
        s.ccsem = nc.alloc_semaphore("cc")
        s.cccount = 0
        s.nops = 0

    def _wait(s, e, toks):
        best = {}
        for t in toks:
            if t is None:
                continue
            sem, val, src = t
            if e == "pe" and src == "pe":
                continue
            key = id(sem)
            if s.waited[e].get(key, 0) >= val:
                continue
            if key not in best or best[key][1] < val:
                best[key] = (sem, val)
        for key, (sem, val) in best.items():
            s.eng[e].wait_ge(sem, val)
            s.waited[e][key] = val

    @staticmethod
    def _deps(R, W):
        toks = []
        for b in R:
            toks.append(b.w)
        for b in W:
            toks.append(b.w)
            toks.extend(b.r.values())
        return toks

    def _finish(s, e, tok, R, W):
        if tok is None:
            for b in R:
                s.pending[e].append(b)
            for b in W:
                b.w = None
                b.r = {}
                s.pending[e].append(("w", b))
            return
        for p in s.pending[e]:
            if isinstance(p, tuple):
                if p[1].w is None:
                    p[1].w = tok
            else:
                p.r[id(tok[0])] = tok
        s.pending[e] = []
        for b in R:
            b.r[id(tok[0])] = tok
        for b in W:
            b.w = tok
            b.r = {}

    def op(s, e, fn, R=(), W=(), signal=True, skip_same=False):
        toks = s._deps(R, W)
        if skip_same:
            toks = [t for t in toks if t is not None and t[2] != e]
        s._wait(e, toks)
        ins = fn(s.eng[e])
        s.nops += 1
        tok = None
        if signal:
            s.ccount[e] += 1
            ins.then_inc(s.csem[e], 1)
            tok = (s.csem[e], s.ccount[e], e)
        s._finish(e, tok, R, W)
        return tok

    def dma(s, q, out, in_, R=(), W=()):
        k = s.dk[q]
        sem = s.dsem[q][k % s.NDS]
        toks = s._deps(R, W)
        if k >= s.NDS:
            toks.append((sem, (k // s.NDS) * 16, "dma"))
        s._wait(q, toks)
        s.eng[q].dma_start(out=out, in_=in_).then_inc(sem, 16)
        s.nops += 1
        s.dk[q] = k + 1
        tok = (sem, (k // s.NDS + 1) * 16, "dma")
        s._finish(q, tok, R, W)
        return tok

    def coll(s, groups, in_ap, out_ap, R=(), W=()):
        s._wait("pool", s._deps(R, W))
        s.nc.gpsimd.collective_compute("AllGather", ALU.bypass, replica_groups=groups,
                                       ins=[in_ap.opt()], outs=[out_ap.opt()]).then_inc(s.ccsem)
        s.cccount += 1
        tok = (s.ccsem, s.cccount, "cc")
        s._finish("pool", tok, R, W)
        return tok

    def all_tokens(s):
        toks = []
        for e in ("pe", "act", "dve", "pool"):
            if s.ccount[e]:
                toks.append((s.csem[e], s.ccount[e], "x"))
        for q in s.dsem:
            k = s.dk[q]
            for j in range(s.NDS):
                if k > j:
                    toks.append((s.dsem[q][j], ((k - j + s.NDS - 1) // s.NDS) * 16, "dma"))
        if s.cccount:
            toks.append((s.ccsem, s.cccount, "cc"))
        return toks

    def barrier(s):
        toks = s.all_tokens()
        for e in ("pe", "act", "dve", "pool", "sp"):
            s._wait(e, toks)


def build_program(cfg):
    nc = bass.Bass("TRN2", target_bir_lowering=False)
    D, T, DFF, L, G, KC, TQ, TT = cfg.D, cfg.T, cfg.DFF, cfg.L, cfg.G, cfg.KC, cfg.TQ, cfg.TT
    Wc, GWc, nch, CW = cfg.Wc, cfg.GWc, cfg.nch, cfg.CW
    NQ, NT = TQ // TT, T // TT
    multi = cfg.multi
    spc, NSP = sp_layout(cfg)
    R1, C1 = cfg.slab
    E = Emit(nc)
    quad = [[0, 1, 2, 3], [4, 5, 6, 7]]
    pairs = [[0, 4], [1, 5], [2, 6], [3, 7]]
    goff = nc.sync.snap((nc.partition_id() % G) * (cfg.NRB * G * 128)) if multi else 0

    def op(e, fn, R=(), W=(), signal=True, skip_same=False):
        return E.op(e, fn, R, W, signal, skip_same)

    ucnt = [0]

    def sbt(name, shape, dt):
        ucnt[0] += 1
        return nc.sbuf_tensor("%s_u%d" % (name, ucnt[0]), shape, dt)

    xT_in = nc.dram_tensor("xT", [D, TQ], F32, kind="ExternalInput").ap()
    outT = nc.dram_tensor("outT", [D, TQ], F32, kind="ExternalOutput").ap()
    out_b = Buf()
    small_in = nc.dram_tensor("small", [128, L * NSP], F32, kind="ExternalInput").ap()
    gfin_in = nc.dram_tensor("gfin", [128, KC], F32, kind="ExternalInput").ap()
    lora_in = nc.dram_tensor("lora", [128, L * 3 * Wc], F32, kind="ExternalInput").ap()
    consts = make_consts()
    cst_in = {nm: nc.dram_tensor("c_" + nm, list(a.shape), F32, kind="ExternalInput").ap() for nm, a in consts.items()}
    Wimg, wprep, wbuf = {}, [], {}
    for l in range(L):
        for nm, r, c, ways in cfg.wdefs:
            npad = cfg.wpad(r, c, ways)
            nsh = npad // ways if multi else npad
            u = nc.dram_tensor("u_%s_%d" % (nm, l), [nsh // C1, C1], F32, kind="ExternalInput")
            f = nc.dram_tensor("f_%s_%d" % (nm, l), [npad], BF16)
            Wimg[(nm, l)] = f.ap()[0:r * c].rearrange("(r c) -> r c", c=c)
            wbuf[(nm, l)] = Buf()
            wprep.append((nm, l, u, f, npad, ways))
    xres = nc.dram_tensor("xres", [D, TQ], F32).ap()
    xres_b = Buf()
    hsrc = nc.dram_tensor("hsrc", [D, TQ], BF16).ap()
    hsrc_b = Buf()
    hg = nc.dram_tensor("hg", [(D // 256) * G * 256, TQ], BF16).ap() if multi else None
    hfull_b = Buf()
    bund = nc.dram_tensor("bund", [G * cfg.NRB * 128, TQ], BF16).ap()

    def bund_dst(r0, r1, tok0, n):
        q, t0 = tok0 // TQ, tok0 % TQ
        return bund[q * cfg.NRB * 128 + r0:q * cfg.NRB * 128 + r1, t0:t0 + n]

    bund_b = Buf()
    bg = nc.dram_tensor("bg", [G * cfg.NRB * G * 128, TQ], BF16).ap() if multi else None
    bgq = nc.dram_tensor("bgq", [cfg.NRB * G * 128, TQ], BF16).ap() if multi else None
    bgat_b = Buf()
    bfull_b = Buf()
    tm = nc.dram_tensor("rw_tm", [(Wc // 64) * T, 320], F32).ap()
    tm_b = Buf()
    fm = nc.dram_tensor("rw_fm", [3 * Wc, T], F32).ap()
    fm_b = Buf()
    hread_b = hfull_b if multi else hsrc_b
    bread_b = bfull_b if multi else bund_b

    small = nc.alloc_sbuf_tensor("small_sb", [128, L * NSP], F32)
    gfin = nc.alloc_sbuf_tensor("gfin_sb", [128, KC], F32)
    lora = nc.alloc_sbuf_tensor("lora_sb", [128, L * 3 * Wc], F32)
    cst = {nm: nc.alloc_sbuf_tensor("cs_" + nm, list(cst_in[nm].shape), F32) for nm in ("ident", "blk64")}
    cstb = {nm: nc.alloc_sbuf_tensor("cb_" + nm, list(cst_in[nm].shape), BF16) for nm in ("ustrict", "lincl", "cmask")}
    onesD = nc.alloc_sbuf_tensor("onesD", [128, 128], BF16)
    blk64s = nc.alloc_sbuf_tensor("blk64s", [128, 128], F32)
    epsc = nc.alloc_sbuf_tensor("epsc", [128, 4], F32)
    cb_ = Buf()
    PS = [nc.alloc_psum_tensor("ps%d" % i, [128, 512], F32) for i in range(8)]
    PSB = [Buf() for _ in range(8)]
    psk = [0]

    def ps_next(lo=0, hi=8):
        i = lo + psk[0] % (hi - lo)
        psk[0] += 1
        return PS[i], PSB[i]

    E.dma("sp", small[:], small_in, W=[cb_])
    E.dma("sp", gfin[:], gfin_in, W=[cb_])
    E.dma("sp", lora[:], lora_in, W=[cb_])
    for nm in cst:
        E.dma("sp", cst[nm][:], cst_in[nm], W=[cb_])
    for nm in cstb:
        E.dma("pool", cstb[nm][:], cst_in[nm], W=[cb_])
    op("dve", lambda e: e.memset(onesD[:], 1.0 / D), W=[cb_])
    op("dve", lambda e: e.memset(epsc[:, 0:1], RMS_EPS), W=[cb_])
    op("dve", lambda e: e.memset(epsc[:, 1:2], LNX_EPS), W=[cb_])
    op("dve", lambda e: e.memset(epsc[:, 2:3], 1e-24), W=[cb_])
    op("dve", lambda e: e.memset(epsc[:, 3:4], -0.5), W=[cb_])
    op("dve", lambda e: e.tensor_scalar(blk64s[:], cst["blk64"][:], 1.0 / 64, None, ALU.mult), R=[cb_], W=[cb_])
    for l in range(L):
        b0 = l * NSP
        for c in range(nch):
            op("dve", lambda e, a=b0 + spc["negw0"] + c, b=b0 + spc["w0"] + c:
               e.tensor_scalar(small[:, a:a + 1], small[:, b:b + 1], -1.0, None, ALU.mult), R=[cb_], W=[cb_])
            op("dve", lambda e, a=b0 + spc["omka"] + c, b=b0 + spc["k_a"] + c:
               e.tensor_scalar(small[:, a:a + 1], small[:, b:b + 1], -1.0, 1.0, ALU.mult, ALU.add), R=[cb_], W=[cb_])

    def sm(l, name, c=0):
        a = l * NSP + spc[name] + c
        return small[:, a:a + 1]

    def prep_weights(l, names=None):
        for nm, ll, u, f, npad, ways in wprep:
            if ll != l or (names is not None and nm not in names):
                continue
            b = wbuf[(nm, l)]
            step = max(1, (1 << 22) // C1)
            if not multi:
                nrow = npad // C1
                for r0 in range(0, nrow, step):
                    r1 = min(nrow, r0 + step)
                    E.dma("pool", f.ap()[r0 * C1:r1 * C1].rearrange("(r c) -> r c", c=C1), u.ap()[r0:r1, :], W=[b])
                continue
            nsh = npad // ways
            nsl = nsh // cfg.SLAB
            ub = nc.dram_tensor("ub_%s_%d" % (nm, l), [nsl * R1, C1], BF16).ap()
            ubb = Buf()
            for r0 in range(0, nsl * R1, step):
                r1 = min(nsl * R1, r0 + step)
                E.dma("pool", ub[r0:r1, :], u.ap()[r0:r1, :], W=[ubb])
            if ways == 2:
                for i in range(nsl):
                    E.coll(pairs, ub[i * R1:(i + 1) * R1, :],
                           f.ap()[2 * i * cfg.SLAB:(2 * i + 2) * cfg.SLAB].rearrange("(r c) -> r c", c=C1), R=[ubb], W=[b])
                continue
            t1 = nc.dram_tensor("t1_%s_%d" % (nm, l), [2 * nsl * R1, C1], BF16).ap()
            t1b = Buf()
            for i in range(nsl):
                E.coll(pairs, ub[i * R1:(i + 1) * R1, :], t1[2 * i * R1:(2 * i + 2) * R1, :], R=[ubb], W=[t1b])
            for j in range(2 * nsl):
                E.coll(quad, t1[j * R1:(j + 1) * R1, :],
                       f.ap()[4 * j * cfg.SLAB:(4 * j + 4) * cfg.SLAB].rearrange("(r c) -> r c", c=C1), R=[t1b], W=[b])

    def hT_src(kc, tok0, n):
        if not multi:
            return hsrc[kc * 128:(kc + 1) * 128, tok0:tok0 + n]
        rank, t0 = tok0 // TQ, tok0 % TQ
        row = ((kc // 2) * G + rank) * 256 + (kc % 2) * 128
        return hg[row:row + 128, t0:t0 + n]

    def bund_src(rank, rb, p0, np_, t0, n):
        if not multi:
            return bund[rb * 128 + p0:rb * 128 + p0 + np_, t0:t0 + n]
        base = rb * G * 128 + rank * 128 + p0
        return bgq[base:base + np_, t0:t0 + n]

    def exchange_h():
        if multi:
            for i in range(D // 256):
                E.coll(quad, hsrc[i * 256:(i + 1) * 256, :], hg[i * G * 256:(i + 1) * G * 256, :], R=[hsrc_b], W=[hfull_b])

    def exchange_bundle():
        if multi:
            for rb in range(cfg.NRB):
                for tq in range(G):
                    k = tq * cfg.NRB + rb
                    E.coll(quad, bund[(tq * cfg.NRB + rb) * 128:(tq * cfg.NRB + rb + 1) * 128, :],
                           bg[k * G * 128:(k + 1) * G * 128, :], R=[bund_b], W=[bgat_b])
            nrow = cfg.NRB * G * 128
            E.dma("sp", bgq, bg[bass.ds(goff, nrow), :], R=[bgat_b], W=[bfull_b])

    def rstd_from_ps(ps, pb, rstd, rb_, eps_col, n=TT):
        op("act", lambda e: e.activation(out=rstd[:, 0:n], in_=ps[:, 0:n], func=AF.Ln, bias=epsc[:, eps_col:eps_col + 1]), R=[pb, cb_], W=[rb_])
        op("act", lambda e: e.activation(out=rstd[:, 0:n], in_=rstd[:, 0:n], func=AF.Exp, scale=-0.5), R=[rb_], W=[rb_])

    def emit_norm0():
        with ExitStack() as st:
            xt = st.enter_context(sbt("n_x", [128, KC, TT], F32))
            sq = st.enter_context(sbt("n_sq", [128, KC, TT], BF16))
            ht = st.enter_context(sbt("n_h", [128, KC, TT], BF16))
            rstd = st.enter_context(sbt("n_r", [128, TT], F32))
            xb, sqb, hb, rb_ = Buf(), Buf(), Buf(), Buf()
            for tt in range(NQ):
                ts = slice(tt * TT, (tt + 1) * TT)
                E.dma("sp", xt[:], xT_in[:, ts].rearrange("(k p) t -> p k t", p=128), W=[xb])
                E.dma("pool", xres[:, ts].rearrange("(k p) t -> p k t", p=128), xt[:], R=[xb], W=[xres_b])
                op("act", lambda e: e.activation(out=sq[:], in_=xt[:], func=AF.Square), R=[xb], W=[sqb])
                ps, pb = ps_next()
                for kc in range(KC):
                    op("pe", lambda e, kc=kc: e.matmul(ps[:], onesD[:], sq[:, kc, :], start=(kc == 0), stop=(kc == KC - 1)),
                       R=[sqb, cb_], W=[pb], signal=(kc == KC - 1))
                rstd_from_ps(ps, pb, rstd, rb_, 0)
                for kc in range(KC):
                    op("dve", lambda e, kc=kc: e.scalar_tensor_tensor(out=ht[:, kc, :], in0=xt[:, kc, :], scalar=sm(0, "gmix", kc),
                                                                    in1=rstd[:], op0=ALU.mult, op1=ALU.mult), R=[xb, rb_, cb_], W=[hb])
                E.dma("pool", hsrc[:, ts].rearrange("(k p) t -> p k t", p=128), ht[:], R=[hb], W=[hsrc_b])
            E.barrier()

    def load_hT(ht, htb, it):
        for kc in range(KC):
            E.dma("sp", ht[:, kc, :], hT_src(kc, it * TT, TT), R=[hread_b], W=[htb])

    def proj_fm(ps, pb, wt, wtb, c0, m, ht, htb):
        for kc in range(KC):
            op("pe", lambda e, kc=kc: e.matmul(ps[0:m, :], wt[:, kc, c0:c0 + m], ht[:, kc, :], start=(kc == 0), stop=(kc == KC - 1)),
               R=[wtb, htb], W=[pb], signal=(kc == KC - 1))

    def sec_conv(l):
        wimg, wb = Wimg[("w_in", l)], wbuf[("w_in", l)]
        ncol = 3 * Wc + GWc
        with ExitStack() as st:
            wt = st.enter_context(sbt("c_w", [128, KC, ncol], BF16))
            hts = [st.enter_context(sbt("c_h%d" % i, [128, KC, TT], BF16)) for i in range(2)]
            M = [st.enter_context(sbt("c_m%d" % c, [128, TT + 2], F32)) for c in range(nch)]
            ccs = st.enter_context(sbt("c_cc", [128, TT], F32))
            acc = st.enter_context(sbt("c_acc", [128, TT], F32))
            yo = [st.enter_context(sbt("c_y%d" % i, [128, TT], BF16)) for i in range(2)]
            wtb, hb, Mb, ccb, accb, yb = Buf(), [Buf(), Buf()], [Buf() for _ in range(nch)], Buf(), Buf(), [Buf(), Buf()]
            E.dma("sp", wt[:, :, 0:3 * Wc], wimg[:, 3 * Wc:6 * Wc].rearrange("(k p) c -> p k c", p=128), R=[wb], W=[wtb])
            E.dma("sp", wt[:, :, 3 * Wc:ncol], wimg[:, cfg.cGD:cfg.cGD + GWc].rearrange("(k p) c -> p k c", p=128), R=[wb], W=[wtb])
            for c in range(nch):
                op("dve", lambda e, c=c: e.memset(M[c][:], 0.0), W=[Mb[c]])
            yk = 0
            for it in range(NT):
                ht, htb = hts[it % 2], hb[it % 2]
                load_hT(ht, htb, it)
                ts = slice(it * TT, (it + 1) * TT)
                for c in range(nch):
                    pcc, pccb = ps_next()
                    proj_fm(pcc, pccb, wt, wtb, Wc + c * 128, 128, ht, htb)
                    op("act", lambda e, pcc=pcc: e.activation(out=ccs[:], in_=pcc[:], func=AF.Copy), R=[pccb], W=[ccb])
                    pcu, pcub = ps_next()
                    proj_fm(pcu, pcub, wt, wtb, 2 * Wc + c * 128, 128, ht, htb)
                    op("dve", lambda e, c=c: e.tensor_copy(M[c][:, 0:2], M[c][:, TT:TT + 2]), R=[Mb[c]], W=[Mb[c]])
                    op("dve", lambda e, c=c, pcu=pcu: e.tensor_tensor(M[c][:, 2:TT + 2], ccs[:], pcu[:], ALU.mult), R=[ccb, pcub, Mb[c]], W=[Mb[c]])
                    cw = lambda j, c=c: sm(l, "convw", j * nch + c)
                    op("dve", lambda e, c=c, cw=cw: e.tensor_scalar(acc[:], M[c][:, 2:TT + 2], cw(2), None, ALU.mult), R=[Mb[c], cb_], W=[accb])
                    op("dve", lambda e, c=c, cw=cw: e.scalar_tensor_tensor(out=acc[:], in0=M[c][:, 1:TT + 1], scalar=cw(1), in1=acc[:], op0=ALU.mult, op1=ALU.add),
                       R=[Mb[c], accb, cb_], W=[accb])
                    op("dve", lambda e, c=c, cw=cw: e.scalar_tensor_tensor(out=acc[:], in0=M[c][:, 0:TT], scalar=cw(0), in1=acc[:], op0=ALU.mult, op1=ALU.add),
                       R=[Mb[c], accb, cb_], W=[accb])
                    pcb, pcbb = ps_next()
                    proj_fm(pcb, pcbb, wt, wtb, c * 128, 128, ht, htb)
                    y, ybb = yo[yk % 2], yb[yk % 2]
                    yk += 1
                    op("dve", lambda e, y=y, pcb=pcb: e.tensor_tensor(y[:], acc[:], pcb[:], ALU.mult), R=[accb, pcbb], W=[ybb])
                    E.dma("pool", bund_dst((nch + c) * 128, (nch + c + 1) * 128, it * TT, TT), y[:], R=[ybb], W=[bund_b])
                for gblk in range((GWc + 127) // 128):
                    m = min(128, GWc - gblk * 128)
                    pg, pgb = ps_next()
                    proj_fm(pg, pgb, wt, wtb, 3 * Wc + gblk * 128, m, ht, htb)
                    y, ybb = yo[yk % 2], yb[yk % 2]
                    yk += 1
                    op("act", lambda e, y=y, pg=pg, m=m: e.activation(out=y[0:m, :], in_=pg[0:m, :], func=AF.Copy), R=[pgb], W=[ybb])
                    r0 = (3 * nch + gblk) * 128
                    E.dma("pool", bund_dst(r0, r0 + m, it * TT, TT), y[0:m, :], R=[ybb], W=[bund_b])
            E.barrier()

    def sec_rwkv_prep(l):
        wimg, wb = Wimg[("w_in", l)], wbuf[("w_in", l)]
        ncol = 3 * Wc + 384
        NV = 3 * nch + 3
        lb = l * 3 * Wc
        with ExitStack() as st:
            wts_ = [st.enter_context(sbt("r_w%d" % i, [128, KC, 128], BF16)) for i in range(2)]
            wtsb_ = [Buf(), Buf()]
            wkk = [0]
            hts = [st.enter_context(sbt("r_h%d" % i, [128, KC, TT], BF16)) for i in range(2)]
            Xs = [st.enter_context(sbt("r_x%d" % i, [128, TT + 1], F32)) for i in range(NV)]
            SG = [st.enter_context(sbt("r_sg%d" % i, [128, TT], F32)) for i in range(NV)]
            names = ["dd", "e1", "dec", "aa", "kk", "k2", "rn", "nkk", "bv", "t1", "kp", "rkr", "bon", "gg"]
            Wk = {n_: st.enter_context(sbt("r_" + n_, [128, TT], F32)) for n_ in names}
            Wb = {n_: Buf() for n_ in names}
            stg = [st.enter_context(sbt("r_st%d" % i, [128, 4, 128], F32)) for i in range(2)]
            stgb = [Buf(), Buf()]
            wtb, hb, Xb, SGb = Buf(), [Buf(), Buf()], [Buf() for _ in range(NV)], [Buf() for _ in range(NV)]
            for i in range(NV):
                op("dve", lambda e, i=i: e.memset(Xs[i][:, TT:TT + 1], 0.0), W=[Xb[i]])
            def mu_of(i):
                if i < nch:
                    return sm(l, "mu_r", i)
                if i < 2 * nch:
                    return sm(l, "mu_k", i - nch)
                if i < 3 * nch:
                    return sm(l, "mu_v", i - 2 * nch)
                if i == 3 * nch:
                    return sm(l, "mu_l")
                return sm(l, "mu_g", i - 3 * nch - 1)
            sk = [0]
            for it in range(NT):
                ht, htb = hts[it % 2], hb[it % 2]
                load_hT(ht, htb, it)
                ts = slice(it * TT, (it + 1) * TT)
                for i in range(NV):
                    wt, wtb = wts_[wkk[0] % 2], wtsb_[wkk[0] % 2]
                    wkk[0] += 1
                    E.dma("sp", wt[:], wimg[:, 6 * Wc + i * 128:6 * Wc + (i + 1) * 128].rearrange("(k p) c -> p k c", p=128), R=[wb], W=[wtb])
                    ps, pb = ps_next()
                    proj_fm(ps, pb, wt, wtb, 0, 128, ht, htb)
                    op("dve", lambda e, i=i: e.tensor_copy(Xs[i][:, 0:1], Xs[i][:, TT:TT + 1]), R=[Xb[i]], W=[Xb[i]])
                    op("act", lambda e, i=i, ps=ps: e.activation(out=Xs[i][:, 1:TT + 1], in_=ps[:], func=AF.Copy), R=[pb, Xb[i]], W=[Xb[i]])
                    op("dve", lambda e, i=i: e.tensor_tensor(Wk["dd"][:], Xs[i][:, 0:TT], Xs[i][:, 1:TT + 1], ALU.subtract), R=[Xb[i]], W=[Wb["dd"]])
                    op("dve", lambda e, i=i: e.scalar_tensor_tensor(out=SG[i][:], in0=Wk["dd"][:], scalar=mu_of(i), in1=Xs[i][:, 1:TT + 1], op0=ALU.mult, op1=ALU.add),
                       R=[Wb["dd"], Xb[i], cb_], W=[SGb[i]])
                iL, iGA, iGB = 3 * nch, 3 * nch + 1, 3 * nch + 2
                op("act", lambda e: e.activation(out=SG[iL][0:64, :], in_=SG[iL][0:64, :], func=AF.Tanh), R=[SGb[iL]], W=[SGb[iL]])
                op("act", lambda e: e.activation(out=SG[iGA][:], in_=SG[iGA][:], func=AF.Sigmoid), R=[SGb[iGA]], W=[SGb[iGA]])
                op("act", lambda e: e.activation(out=SG[iGB][0:32, :], in_=SG[iGB][0:32, :], func=AF.Sigmoid), R=[SGb[iGB]], W=[SGb[iGB]])
                for c in range(nch):
                    R_, K_, V_ = SG[c], SG[nch + c], SG[2 * nch + c]
                    Rb, Kb, Vb = SGb[c], SGb[nch + c], SGb[2 * nch + c]
                    cs = slice(c * 128, (c + 1) * 128)
                    ps, pb = ps_next()
                    op("pe", lambda e, ps=ps: e.matmul(ps[:], lora[0:64, lb + c * 128:lb + (c + 1) * 128], SG[iL][0:64, :], start=True, stop=True), R=[SGb[iL], cb_], W=[pb])
                    op("act", lambda e, ps=ps: e.activation(out=Wk["e1"][:], in_=ps[:], func=AF.Exp, scale=-1.0, bias=sm(l, "negw0", c)), R=[pb, cb_], W=[Wb["e1"]])
                    op("act", lambda e: e.activation(out=Wk["e1"][:], in_=Wk["e1"][:], func=AF.Ln, bias=1.0), R=[Wb["e1"]], W=[Wb["e1"]])
                    op("act", lambda e: e.activation(out=Wk["e1"][:], in_=Wk["e1"][:], func=AF.Exp, scale=-1.0, bias=epsc[:, 3:4]), R=[Wb["e1"], cb_], W=[Wb["e1"]])
                    op("act", lambda e: e.activation(out=Wk["dec"][:], in_=Wk["e1"][:], func=AF.Exp, scale=-1.0), R=[Wb["e1"]], W=[Wb["dec"]])
                    ps, pb = ps_next()
                    op("pe", lambda e, ps=ps: e.matmul(ps[:], lora[64:128, lb + c * 128:lb + (c + 1) * 128], SG[iL][64:128, :], start=True, stop=True), R=[SGb[iL], cb_], W=[pb])
                    op("act", lambda e, ps=ps: e.activation(out=Wk["aa"][:], in_=ps[:], func=AF.Sigmoid, bias=sm(l, "a0", c)), R=[pb, cb_], W=[Wb["aa"]])
                    ps, pb = ps_next()
                    op("pe", lambda e, ps=ps: e.matmul(ps[:], lora[:, lb + Wc + c * 128:lb + Wc + (c + 1) * 128], SG[iGA][:], start=True, stop=False), R=[SGb[iGA], cb_], W=[pb], signal=False)
                    op("pe", lambda e, ps=ps: e.matmul(ps[:], lora[0:32, lb + 2 * Wc + c * 128:lb + 2 * Wc + (c + 1) * 128], SG[iGB][0:32, :], start=False, stop=True), R=[SGb[iGB], cb_], W=[pb])
                    op("act", lambda e, ps=ps: e.activation(out=Wk["gg"][:], in_=ps[:], func=AF.Copy), R=[pb], W=[Wb["gg"]])
                    op("dve", lambda e: e.tensor_scalar(Wk["kk"][:], K_[:], sm(l, "k_k", c), None, ALU.mult), R=[Kb, cb_], W=[Wb["kk"]])
                    op("dve", lambda e: e.tensor_tensor(Wk["k2"][:], Wk["kk"][:], Wk["kk"][:], ALU.mult), R=[Wb["kk"]], W=[Wb["k2"]])
                    ps, pb = ps_next()
                    op("pe", lambda e, ps=ps: e.matmul(ps[:], cst["blk64"][:], Wk["k2"][:], start=True, stop=True), R=[Wb["k2"], cb_], W=[pb])
                    rstd_from_ps(ps, pb, Wk["rn"], Wb["rn"], 2)
                    op("dve", lambda e: e.scalar_tensor_tensor(out=Wk["nkk"][:], in0=Wk["kk"][:], scalar=-1.0, in1=Wk["rn"][:], op0=ALU.mult, op1=ALU.mult), R=[Wb["kk"], Wb["rn"]], W=[Wb["nkk"]])
                    op("dve", lambda e: e.scalar_tensor_tensor(out=Wk["bv"][:], in0=Wk["nkk"][:], scalar=-1.0, in1=Wk["aa"][:], op0=ALU.mult, op1=ALU.mult), R=[Wb["nkk"], Wb["aa"]], W=[Wb["bv"]])
                    op("dve", lambda e: e.tensor_scalar(Wk["t1"][:], Wk["aa"][:], sm(l, "k_a", c), sm(l, "omka", c), ALU.mult, ALU.add), R=[Wb["aa"], cb_], W=[Wb["t1"]])
                    op("dve", lambda e: e.tensor_tensor(Wk["kp"][:], K_[:], Wk["t1"][:], ALU.mult), R=[Kb, Wb["t1"]], W=[Wb["kp"]])
                    op("dve", lambda e: e.scalar_tensor_tensor(out=Wk["rkr"][:], in0=R_[:], scalar=sm(l, "r_k", c), in1=Wk["kp"][:], op0=ALU.mult, op1=ALU.mult), R=[Rb, Wb["kp"], cb_], W=[Wb["rkr"]])
                    ps, pb = ps_next()
                    op("pe", lambda e, ps=ps: e.matmul(ps[:], cst["blk64"][:], Wk["rkr"][:], start=True, stop=True), R=[Wb["rkr"], cb_], W=[pb])
                    op("dve", lambda e, ps=ps: e.tensor_tensor(Wk["bon"][:], V_[:], ps[:], ALU.mult), R=[Vb, pb], W=[Wb["bon"]])
                    op("dve", lambda e: e.tensor_tensor(Wk["bon"][:], Wk["bon"][:], Wk["gg"][:], ALU.mult), R=[Wb["bon"], Wb["gg"]], W=[Wb["bon"]])
                    E.dma("pool", fm[0 * Wc + c * 128:0 * Wc + (c + 1) * 128, ts], V_[:], R=[Vb], W=[fm_b])
                    E.dma("pool", fm[1 * Wc + c * 128:1 * Wc + (c + 1) * 128, ts], Wk["gg"][:], R=[Wb["gg"]], W=[fm_b])
                    E.dma("pool", fm[2 * Wc + c * 128:2 * Wc + (c + 1) * 128, ts], Wk["bon"][:], R=[Wb["bon"]], W=[fm_b])
                    for vi, (src, srcb) in enumerate(((R_, Rb), (Wk["dec"], Wb["dec"]), (Wk["kp"], Wb["kp"]), (Wk["nkk"], Wb["nkk"]), (Wk["bv"], Wb["bv"]))):
                        ps, pb = ps_next()
                        for tb in range(TT // 128):
                            op("pe", lambda e, ps=ps, tb=tb, src=src: e.transpose(ps[:, tb * 128:(tb + 1) * 128], src[:, tb * 128:(tb + 1) * 128], cst["ident"][:]),
                               R=[srcb, cb_], W=[pb], signal=(tb == TT // 128 - 1))
                        sg_, sgb_ = stg[sk[0] % 2], stgb[sk[0] % 2]
                        sk[0] += 1
                        op("act" if vi % 2 else "dve",
                           (lambda e, ps=ps, sg_=sg_: e.activation(out=sg_[:], in_=ps[:].rearrange("p (a b) -> p a b", b=128), func=AF.Copy)) if vi % 2 else
                           (lambda e, ps=ps, sg_=sg_: e.tensor_copy(sg_[:], ps[:].rearrange("p (a b) -> p a b", b=128))), R=[pb], W=[sgb_])
                        for hl in range(2):
                            r0 = (2 * c + hl) * T + it * TT
                            E.dma("pool", tm[r0:r0 + TT, vi * 64:(vi + 1) * 64].rearrange("(a p) k -> p a k", p=128), sg_[:, :, hl * 64:(hl + 1) * 64], R=[sgb_], W=[tm_b])
            E.barrier()

    def sec_rwkv_scan(l):
        NB = TT // NREP
        for tp in range(0, nch, 2):
            tiles = list(range(tp, min(nch, tp + 2)))
            ss = len(tiles) == 2
            with ExitStack() as st:
                rep = {(c, i): st.enter_context(sbt("s_rep%d_%d" % (c, i), [128, NREP, 5, 64], F32)) for c in tiles for i in range(2)}
                repb = {k: Buf() for k in rep}
                vcol = {(c, i): st.enter_context(sbt("s_v%d_%d" % (c, i), [128, TT], F32)) for c in tiles for i in range(2)}
                vcb = {k: Buf() for k in vcol}
                S = {c: st.enter_context(sbt("s_S%d" % c, [128, 64], F32)) for c in tiles}
                tmp = {c: st.enter_context(sbt("s_t%d" % c, [128, 64], F32)) for c in tiles}
                sa = {c: st.enter_context(sbt("s_sa%d" % c, [128, 1], F32)) for c in tiles}
                Y = {c: st.enter_context(sbt("s_Y%d" % c, [128, TT], F32)) for c in tiles}
                Sb, tmpb, sab, Yb = ({c: Buf() for c in tiles} for _ in range(4))
                pw = {n_: st.enter_context(sbt("s_" + n_, [128, TT], F32)) for n_ in ("yc", "sq", "rs", "g", "bg")}
                pwb = {n_: Buf() for n_ in pw}
                ob = [st.enter_context(sbt("s_o%d" % i, [128, TT], BF16)) for i in range(2)]
                obb = [Buf(), Buf()]
                ok = [0]
                for c in tiles:
                    op("dve", lambda e, c=c: e.memset(S[c][:], 0.0), W=[Sb[c]])
                for it in range(NT):
                    for c in tiles:
                        v_, vb_ = vcol[(c, it % 2)], vcb[(c, it % 2)]
                        E.dma("sp", v_[:], fm[c * 128:(c + 1) * 128, it * TT:(it + 1) * TT], R=[fm_b], W=[vb_])
                    for blk in range(NB):
                        gb = it * NB + blk
                        t0 = it * TT + blk * NREP
                        for c in tiles:
                            rp, rpb = rep[(c, gb % 2)], repb[(c, gb % 2)]
                            for hl in range(2):
                                r0 = (2 * c + hl) * T + t0
                                E.dma("sp", rp[hl * 64:(hl + 1) * 64].rearrange("p t v k -> p t (v k)"), tm[r0:r0 + NREP, :].partition_broadcast(64), R=[tm_b], W=[rpb])
                        for j in range(NREP):
                            tc = blk * NREP + j
                            for c in tiles:
                                rp, rpb = rep[(c, gb % 2)], repb[(c, gb % 2)]
                                v_, vb_ = vcol[(c, it % 2)], vcb[(c, it % 2)]
                                op("dve", lambda e, c=c, rp=rp, j=j: e.scalar_tensor_tensor(out=tmp[c][:], in0=S[c][:], scalar=1.0, in1=rp[:, j, 3, :], op0=ALU.mult, op1=ALU.mult, accum_out=sa[c][:]),
                                   R=[Sb[c], rpb], W=[tmpb[c], sab[c]], skip_same=ss)
                                op("dve", lambda e, c=c, rp=rp, j=j: e.tensor_tensor(S[c][:], S[c][:], rp[:, j, 1, :], ALU.mult), R=[Sb[c], rpb], W=[Sb[c]], skip_same=ss)
                                op("dve", lambda e, c=c, rp=rp, j=j: e.scalar_tensor_tensor(out=S[c][:], in0=rp[:, j, 4, :], scalar=sa[c][:, 0:1], in1=S[c][:], op0=ALU.mult, op1=ALU.add),
                                   R=[Sb[c], rpb, sab[c]], W=[Sb[c]], skip_same=ss)
                                op("dve", lambda e, c=c, rp=rp, j=j, v_=v_, tc=tc: e.scalar_tensor_tensor(out=S[c][:], in0=rp[:, j, 2, :], scalar=v_[:, tc:tc + 1], in1=S[c][:], op0=ALU.mult, op1=ALU.add),
                                   R=[Sb[c], rpb, vb_], W=[Sb[c]], skip_same=ss)
                                op("dve", lambda e, c=c, rp=rp, j=j, tc=tc: e.scalar_tensor_tensor(out=tmp[c][:], in0=S[c][:], scalar=1.0, in1=rp[:, j, 0, :], op0=ALU.mult, op1=ALU.mult, accum_out=Y[c][:, tc:tc + 1]),
                                   R=[Sb[c], rpb], W=[tmpb[c], Yb[c]], skip_same=ss)
                    ts = slice(it * TT, (it + 1) * TT)
                    for c in tiles:
                        E.dma("sp", pw["g"][:], fm[Wc + c * 128:Wc + (c + 1) * 128, ts], R=[fm_b], W=[pwb["g"]])
                        E.dma("sp", pw["bg"][:], fm[2 * Wc + c * 128:2 * Wc + (c + 1) * 128, ts], R=[fm_b], W=[pwb["bg"]])
                        ps, pb = ps_next()
                        op("pe", lambda e, ps=ps, c=c: e.matmul(ps[:], blk64s[:], Y[c][:], start=True, stop=True), R=[Yb[c], cb_], W=[pb])
                        op("dve", lambda e, ps=ps, c=c: e.tensor_tensor(pw["yc"][:], Y[c][:], ps[:], ALU.subtract), R=[Yb[c], pb], W=[pwb["yc"]])
                        op("act", lambda e: e.activation(out=pw["sq"][:], in_=pw["yc"][:], func=AF.Square), R=[pwb["yc"]], W=[pwb["sq"]])
                        ps, pb = ps_next()
                        op("pe", lambda e, ps=ps: e.matmul(ps[:], blk64s[:], pw["sq"][:], start=True, stop=True), R=[pwb["sq"], cb_], W=[pb])
                        rstd_from_ps(ps, pb, pw["rs"], pwb["rs"], 1)
                        op("dve", lambda e: e.tensor_tensor(pw["yc"][:], pw["yc"][:], pw["rs"][:], ALU.mult), R=[pwb["yc"], pwb["rs"]], W=[pwb["yc"]])
                        op("dve", lambda e, c=c: e.tensor_scalar(pw["yc"][:], pw["yc"][:], sm(l, "lnx_w", c), sm(l, "lnx_b", c), ALU.mult, ALU.add), R=[pwb["yc"], cb_], W=[pwb["yc"]])
                        op("dve", lambda e: e.tensor_tensor(pw["yc"][:], pw["yc"][:], pw["g"][:], ALU.mult), R=[pwb["yc"], pwb["g"]], W=[pwb["yc"]])
                        o_, ob_ = ob[ok[0] % 2], obb[ok[0] % 2]
                        ok[0] += 1
                        op("dve", lambda e, o_=o_: e.tensor_tensor(o_[:], pw["yc"][:], pw["bg"][:], ALU.add), R=[pwb["yc"], pwb["bg"]], W=[ob_])
                        E.dma("pool", bund_dst((2 * nch + c) * 128, (2 * nch + c + 1) * 128, it * TT, TT), o_[:], R=[ob_], W=[bund_b])
                E.barrier()

    def sec_attention(l, hloc):
        scale = 128 ** -0.5
        wimg, wb = Wimg[("w_in", l)], wbuf[("w_in", l)]
        NKB = T // 128
        with ExitStack() as st:
            qT = st.enter_context(sbt("a_q", [128, T], BF16))
            kT = st.enter_context(sbt("a_k", [128, T], BF16))
            vt = st.enter_context(sbt("a_v", [128, NKB, 128], BF16))
            qb, kb_, vb = Buf(), Buf(), Buf()
            with ExitStack() as st2:
                wt = st2.enter_context(sbt("a_w", [128, KC, 384], BF16))
                hts = [st2.enter_context(sbt("a_h%d" % i, [128, KC, TT], BF16)) for i in range(2)]
                wtb, hb = Buf(), [Buf(), Buf()]
                for j in range(3):
                    E.dma("sp", wt[:, :, j * 128:(j + 1) * 128], wimg[:, j * Wc + hloc * 128:j * Wc + (hloc + 1) * 128].rearrange("(k p) c -> p k c", p=128), R=[wb], W=[wtb])
                for it in range(NT):
                    ht, htb = hts[it % 2], hb[it % 2]
                    load_hT(ht, htb, it)
                    ts = slice(it * TT, (it + 1) * TT)
                    ps, pb = ps_next(0, 4)
                    proj_fm(ps, pb, wt, wtb, 0, 128, ht, htb)
                    op("act", lambda e: e.activation(out=qT[:, ts], in_=ps[:], func=AF.Copy), R=[pb], W=[qb])
                    ps, pb = ps_next(0, 4)
                    proj_fm(ps, pb, wt, wtb, 128, 128, ht, htb)
                    op("dve", lambda e: e.tensor_copy(kT[:, ts], ps[:]), R=[pb], W=[kb_])
                    ps, pb = ps_next(0, 4)
                    for tb in range(TT // 128):
                        for kc in range(KC):
                            op("pe", lambda e: e.matmul(ps[:, tb * 128:(tb + 1) * 128], ht[:, kc, tb * 128:(tb + 1) * 128], wt[:, kc, 256:384],
                                                        start=(kc == 0), stop=(kc == KC - 1)),
                               R=[wtb, htb], W=[pb], signal=(kc == KC - 1 and tb == TT // 128 - 1))
                    op("act", lambda e: e.activation(out=vt[:, it * 4:(it + 1) * 4, :], in_=ps[:].rearrange("p (a b) -> p a b", b=128), func=AF.Copy), R=[pb], W=[vb])
                E.barrier()
            with ExitStack() as st2:
                mk = lambda nm, dt: [st2.enter_context(sbt("a_%s%d" % (nm, i), [128, 512], dt)) for i in range(2)]
                et, spt, spb, tt_, wbt, ot = mk("e", F32), mk("sp", F32), mk("spb", BF16), mk("t", F32), mk("wb", BF16), mk("o", BF16)
                eb, spbuf, spbb, tb_, wbb, ob = ([Buf(), Buf()] for _ in range(6))
                ACC, ACCB = PS[4], PSB[4]
                pair = 0
                for qt in range(NT):
                    qs = slice(qt * 512, (qt + 1) * 512)
                    ops, opsb = PS[5 + qt % 2], PSB[5 + qt % 2]
                    kbs = list(range(4 * qt + 3, -1, -1))
                    for ii, kb in enumerate(kbs):
                        s_ = pair % 2
                        pair += 1
                        first, last = ii == 0, ii == len(kbs) - 1
                        mi = kb - 4 * qt
                        zs, zsb = ps_next(0, 4)
                        op("pe", lambda e: e.matmul(zs[:], kT[:, kb * 128:(kb + 1) * 128], qT[:, qs], start=True, stop=True), R=[kb_, qb], W=[zsb])
                        op("act", lambda e: e.activation(out=et[s_][:], in_=zs[:], func=AF.Exp, scale=scale), R=[zsb], W=[eb[s_]])
                        op("act", lambda e: e.activation(out=spt[s_][:], in_=et[s_][:], func=AF.Ln, bias=1.0), R=[eb[s_]], W=[spbuf[s_]])
                        if mi >= 0:
                            op("pool", lambda e: e.tensor_tensor(spb[s_][:], spt[s_][:], cstb["cmask"][:, mi * 512:(mi + 1) * 512], ALU.mult), R=[spbuf[s_], cb_], W=[spbb[s_]])
                        else:
                            op("pool", lambda e: e.tensor_copy(spb[s_][:], spt[s_][:]), R=[spbuf[s_]], W=[spbb[s_]])
                        op("pe", lambda e: e.matmul(ACC[:], cstb["ustrict"][:], spb[s_][:], start=first, stop=False), R=[spbb[s_], cb_], W=[ACCB])
                        op("dve", lambda e: e.scalar_tensor_tensor(out=tt_[s_][:], in0=zs[:], scalar=scale, in1=spt[s_][:], op0=ALU.mult, op1=ALU.subtract),
                           R=[zsb, spbuf[s_]], W=[tb_[s_]])
                        op("dve", lambda e: e.tensor_tensor(tt_[s_][:], tt_[s_][:], ACC[:], ALU.subtract), R=[tb_[s_], ACCB], W=[tb_[s_]])
                        op("pe", lambda e: e.matmul(ACC[:], cstb["lincl"][:], spb[s_][:], start=False, stop=last), R=[spbb[s_], cb_], W=[ACCB])
                        op("act", lambda e: e.activation(out=wbt[s_][:], in_=tt_[s_][:], func=AF.Exp), R=[tb_[s_]], W=[wbb[s_]])
                        if mi >= 0:
                            op("pool", lambda e: e.tensor_tensor(wbt[s_][:], wbt[s_][:], cstb["cmask"][:, mi * 512:(mi + 1) * 512], ALU.mult), R=[wbb[s_], cb_], W=[wbb[s_]])
                        op("pe", lambda e: e.matmul(ops[:], vt[:, kb, :], wbt[s_][:], start=first, stop=last), R=[wbb[s_], vb], W=[opsb])
                    o_s = qt % 2
                    op("act", lambda e: e.activation(out=ot[o_s][:], in_=ops[:], func=AF.Copy), R=[opsb], W=[ob[o_s]])
                    E.dma("pool", bund_dst(hloc * 128, (hloc + 1) * 128, qt * 512, 512), ot[o_s][:], R=[ob[o_s]], W=[bund_b])
                E.barrier()

    def phase_channel(l):
        final = (l == L - 1)
        NFB = DFF // 128
        GS = min(16, NFB)
        NG = NFB // GS
        KW = max(KC, GS, 30)
        wo = [Wimg[(n_, l)] for n_ in ("w_att_o", "w_conv_o", "w_rwkv_o")]
        wob = [wbuf[(n_, l)] for n_ in ("w_att_o", "w_conv_o", "w_rwkv_o")]
        wg, wgb = Wimg[("w_gate_up", l)], wbuf[("w_gate_up", l)]
        wout, woutb = Wimg[("w_out", l)], wbuf[("w_out", l)]
        w1, w1b = Wimg[("w_mlp_up", l)], wbuf[("w_mlp_up", l)]
        w2, w2b = Wimg[("w_mlp_down", l)], wbuf[("w_mlp_down", l)]
        NU = max(26, GS)
        with ExitStack() as st:
            U = st.enter_context(sbt("p_u", [128, NU, TT], BF16))
            MH = st.enter_context(sbt("p_mh", [128, KC, TT], BF16))
            XN = st.enter_context(sbt("p_xn", [128, KC, TT], F32))
            wts = [st.enter_context(sbt("p_w%d" % i, [128, KW, 256], BF16)) for i in range(2)]
            sqc = [st.enter_context(sbt("p_sq%d" % i, [128, TT], BF16)) for i in range(2)]
            sqcb = [Buf(), Buf()]
            ok_ = [0]
            gs = [st.enter_context(sbt("p_gs%d" % i, [128, TT], F32)) for i in range(2)]
            macc = st.enter_context(sbt("p_macc", [128, TT], F32))
            tmp = st.enter_context(sbt("p_tmp", [128, TT], F32))
            rl = [st.enter_context(sbt("p_rl%d" % i, [128, TT], F32)) for i in range(2)]
            rstd = st.enter_context(sbt("p_rs", [128, TT], F32))
            Ub, MHb, XNb = Buf(), Buf(), Buf()
            wtsb, gsb, rlb = ([Buf(), Buf()] for _ in range(3))
            maccb, tmpb, rsb = Buf(), Buf(), Buf()
            wk, gk, rk = [0], [0], [0]

            def norm_to(gain_fn, dst_bf16):
                ps, pb = ps_next()
                for kc in range(KC):
                    sq_, sqb_ = sqc[kc % 2], sqcb[kc % 2]
                    op("act", lambda e: e.activation(out=sq_[:], in_=XN[:, kc, :], func=AF.Square), R=[XNb], W=[sqb_])
                    op("pe", lambda e: e.matmul(ps[:], onesD[:], sq_[:], start=(kc == 0), stop=(kc == KC - 1)), R=[sqb_, cb_], W=[pb])
                rstd_from_ps(ps, pb, rstd, rsb, 0)
                for kc in range(KC):
                    if dst_bf16:
                        op("dve", lambda e: e.scalar_tensor_tensor(out=MH[:, kc, :], in0=XN[:, kc, :], scalar=gain_fn(kc), in1=rstd[:], op0=ALU.mult, op1=ALU.mult),
                           R=[XNb, rsb, cb_], W=[MHb])
                    else:
                        op("dve", lambda e: e.scalar_tensor_tensor(out=XN[:, kc, :], in0=XN[:, kc, :], scalar=gain_fn(kc), in1=rstd[:], op0=ALU.mult, op1=ALU.mult),
                           R=[XNb, rsb, cb_], W=[XNb])

            for tt in range(NQ):
                ts = slice(tt * TT, (tt + 1) * TT)
                t0 = tt * TT
                for br in range(3):
                    for kc in range(8):
                        f0 = kc * 128
                        rank, loc = f0 // Wc, f0 % Wc
                        E.dma("sp", U[:, br * 8 + kc, :], bund_src(rank, br * nch + loc // 128, 0, 128, t0, TT), R=[bread_b], W=[Ub])
                for kc in range(2):
                    pw_ = min(128, GWc)
                    for piece in range(128 // pw_):
                        f0 = kc * 128 + piece * pw_
                        rank, loc = f0 // GWc, f0 % GWc
                        E.dma("sp", U[piece * pw_:(piece + 1) * pw_, 24 + kc, :], bund_src(rank, 3 * nch + loc // 128, loc % 128, pw_, t0, TT), R=[bread_b], W=[Ub])
                E.dma("sp", XN[:], xres[:, ts].rearrange("(k p) t -> p k t", p=128), R=[xres_b], W=[XNb])
                for cp in range(KC // 2):
                    wt, wtb = wts[wk[0] % 2], wtsb[wk[0] % 2]
                    wk[0] += 1
                    flat = wt[:].rearrange("p k c -> p (k c)")
                    wov = [flat[:, br * 2048:(br + 1) * 2048].rearrange("p (k c) -> p k c", c=256) for br in range(3)]
                    wgv = [flat[:, 6144 + br * 512:6144 + (br + 1) * 512].rearrange("p (k c) -> p k c", c=256) for br in range(3)]
                    for br in range(3):
                        E.dma("sp", wov[br], wo[br][:, cp * 256:(cp + 1) * 256].rearrange("(k p) c -> p k c", p=128), R=[wob[br]], W=[wtb])
                        E.dma("sp", wgv[br], wg[:, br * D + cp * 256:br * D + (cp + 1) * 256].rearrange("(k p) c -> p k c", p=128), R=[wgb], W=[wtb])
                    for sub in range(2):
                        cb = 2 * cp + sub
                        cs = slice(sub * 128, (sub + 1) * 128)
                        for br in range(3):
                            yps, ypb = ps_next()
                            for kc in range(8):
                                op("pe", lambda e: e.matmul(yps[:], wov[br][:, kc, cs], U[:, br * 8 + kc, :], start=(kc == 0), stop=(kc == 7)), R=[wtb, Ub], W=[ypb], signal=(kc == 7))
                            gps, gpb = ps_next()
                            for kc in range(2):
                                op("pe", lambda e: e.matmul(gps[:], wgv[br][:, kc, cs], U[:, 24 + kc, :], start=(kc == 0), stop=(kc == 1)), R=[wtb, Ub], W=[gpb], signal=(kc == 1))
                            g_, gb_ = gs[gk[0] % 2], gsb[gk[0] % 2]
                            gk[0] += 1
                            op("act", lambda e: e.activation(out=g_[:], in_=gps[:], func=AF.Sigmoid, bias=sm(l, "bgate", br * KC + cb)), R=[gpb, cb_], W=[gb_])
                            if br == 0:
                                op("dve", lambda e: e.tensor_tensor(macc[:], g_[:], yps[:], ALU.mult), R=[gb_, ypb], W=[maccb])
                            else:
                                op("dve", lambda e: e.tensor_tensor(tmp[:], g_[:], yps[:], ALU.mult), R=[gb_, ypb], W=[tmpb])
                                if br == 1:
                                    op("dve", lambda e: e.tensor_tensor(macc[:], macc[:], tmp[:], ALU.add), R=[maccb, tmpb], W=[maccb])
                                else:
                                    op("dve", lambda e: e.tensor_tensor(MH[:, cb, :], macc[:], tmp[:], ALU.add), R=[maccb, tmpb], W=[MHb])
                for cp in range(KC // 2):
                    wt, wtb = wts[wk[0] % 2], wtsb[wk[0] % 2]
                    wk[0] += 1
                    E.dma("sp", wt[:, 0:KC, :], wout[:, cp * 256:(cp + 1) * 256].rearrange("(k p) c -> p k c", p=128), R=[woutb], W=[wtb])
                    for sub in range(2):
                        cb = 2 * cp + sub
                        ps, pb = ps_next()
                        for kc in range(KC):
                            op("pe", lambda e: e.matmul(ps[:], wt[:, kc, sub * 128:(sub + 1) * 128], MH[:, kc, :], start=(kc == 0), stop=(kc == KC - 1)), R=[wtb, MHb], W=[pb], signal=(kc == KC - 1))
                        op("dve", lambda e: e.tensor_tensor(XN[:, cb, :], XN[:, cb, :], ps[:], ALU.add), R=[XNb, pb], W=[XNb])
                norm_to(lambda kc: sm(l, "gmlp", kc), True)
                for grp in range(NG):
                    for cp2 in range(GS // 2):
                        wt, wtb = wts[wk[0] % 2], wtsb[wk[0] % 2]
                        wk[0] += 1
                        c0 = (grp * GS + 2 * cp2) * 128
                        E.dma("sp", wt[:, 0:KC, :], w1[:, c0:c0 + 256].rearrange("(k p) c -> p k c", p=128), R=[w1b], W=[wtb])
                        for sub in range(2):
                            j = 2 * cp2 + sub
                            ps, pb = ps_next()
                            for kc in range(KC):
                                op("pe", lambda e: e.matmul(ps[:], wt[:, kc, sub * 128:(sub + 1) * 128], MH[:, kc, :], start=(kc == 0), stop=(kc == KC - 1)), R=[wtb, MHb], W=[pb], signal=(kc == KC - 1))
                            r_, rb_ = rl[rk[0] % 2], rlb[rk[0] % 2]
                            rk[0] += 1
                            op("act", lambda e: e.activation(out=r_[:], in_=ps[:], func=AF.Relu), R=[pb], W=[rb_])
                            op("pool", lambda e: e.tensor_tensor(U[:, j, :], r_[:], r_[:], ALU.mult), R=[rb_], W=[Ub])
                    for cp in range(KC // 2):
                        wt, wtb = wts[wk[0] % 2], wtsb[wk[0] % 2]
                        wk[0] += 1
                        r0 = grp * GS * 128
                        E.dma("sp", wt[:, 0:GS, :], w2[r0:r0 + GS * 128, cp * 256:(cp + 1) * 256].rearrange("(k p) c -> p k c", p=128), R=[w2b], W=[wtb])
                        for sub in range(2):
                            cb = 2 * cp + sub
                            ps, pb = ps_next()
                            for j in range(GS):
                                op("pe", lambda e: e.matmul(ps[:], wt[:, j, sub * 128:(sub + 1) * 128], U[:, j, :], start=(j == 0), stop=(j == GS - 1)), R=[wtb, Ub], W=[pb], signal=(j == GS - 1))
                            op("dve", lambda e: e.tensor_tensor(XN[:, cb, :], XN[:, cb, :], ps[:], ALU.add), R=[XNb, pb], W=[XNb])
                if not final:
                    E.dma("pool", xres[:, ts].rearrange("(k p) t -> p k t", p=128), XN[:], R=[XNb], W=[xres_b])
                    norm_to(lambda kc: sm(l + 1, "gmix", kc), True)
                    E.dma("pool", hsrc[:, ts].rearrange("(k p) t -> p k t", p=128), MH[:], R=[MHb], W=[hsrc_b])
                else:
                    norm_to(lambda kc: gfin[:, kc:kc + 1], False)
                    E.dma("pool", outT[:, ts].rearrange("(k p) t -> p k t", p=128), XN[:], R=[XNb], W=[out_b])
            E.barrier()

    prep_weights(0, ["w_in"])
    emit_norm0()
    prep_weights(0, [n_ for n_, _, _, _ in cfg.wdefs if n_ != "w_in"])
    for l in range(L):
        exchange_h()
        if l + 1 < L:
            prep_weights(l + 1)
        sec_conv(l)
        sec_rwkv_prep(l)
        sec_rwkv_scan(l)
        for hloc in range(nch):
            sec_attention(l, hloc)
        exchange_bundle()
        phase_channel(l)
    E.barrier()
    return nc, E


def shard_weight(cfg, wflat_padded, ways, b, g):
    S = cfg.SLAB
    if not cfg.multi:
        return wflat_padded.reshape(-1, cfg.slab[1])
    if ways == 2:
        return np.ascontiguousarray(wflat_padded.reshape(-1, 2, S)[:, b, :]).reshape(-1, cfg.slab[1])
    return np.ascontiguousarray(wflat_padded.reshape(-1, 2, 4, S)[:, b, g, :]).reshape(-1, cfg.slab[1])


def make_in_maps(cfg, inp):
    L, G, B, D, TQ = cfg.L, cfg.G, cfg.B, cfg.D, cfg.TQ
    consts = make_consts()
    x = np.asarray(inp["x"], np.float32)
    in_maps = [dict() for _ in range(cfg.NC)]
    for b in range(B):
        for g in range(G):
            m = in_maps[b * G + g]
            m["xT"] = np.ascontiguousarray(x[b, g * TQ:(g + 1) * TQ, :].T)
            m["small"] = np.concatenate([pack_small(cfg, inp, l, g) for l in range(L)], axis=1)
            m["gfin"] = col128(inp["norm_final"])
            m["lora"] = np.concatenate([pack_lora(cfg, inp, l, g) for l in range(L)], axis=1)
            for nm, a in consts.items():
                m["c_" + nm] = a
    for l in range(L):
        for nm, r, c, ways in cfg.wdefs:
            npad = cfg.wpad(r, c, ways)
            if nm == "w_in":
                w = np.asarray(inp["w_in"][l], np.float32)
                for g in range(G):
                    flat = np.zeros((npad,), np.float32)
                    flat[:r * c] = win_core(cfg, w, g).reshape(-1)
                    for b in range(B):
                        in_maps[b * G + g]["u_%s_%d" % (nm, l)] = shard_weight(cfg, flat, ways, b, g)
            else:
                flat = np.zeros((npad,), np.float32)
                flat[:r * c] = np.asarray(inp[nm][l], np.float32).reshape(-1)
                for b in range(B):
                    for g in range(G):
                        in_maps[b * G + g]["u_%s_%d" % (nm, l)] = shard_weight(cfg, flat, ways, b, g)
    return in_maps


_PROG = {}


def run_cfg(cfg, inp):
    key = (cfg.D, cfg.T, cfg.DFF, cfg.L, cfg.B, cfg.G, cfg.slab)
    if key not in _PROG:
        _PROG[key] = build_program(cfg)[0]
    nc = _PROG[key]
    in_maps = make_in_maps(cfg, inp)
    res = run_bass_kernel_spmd(nc, in_maps, core_ids=list(range(cfg.NC)))
    out = np.zeros((cfg.B, cfg.T, cfg.D), np.float32)
    for b in range(cfg.B):
        for g in range(cfg.G):
            out[b, g * cfg.TQ:(g + 1) * cfg.TQ, :] = res.results[b * cfg.G + g]["outT"].T
    return out


def kernel(**inputs):
    cfg = Cfg()
    return run_cfg(cfg, inputs)
```
